# Optimizing a Trainium2 kernel written in Bass

```python
import math
import jax, jax.numpy as jnp
from jax import lax
import numpy as np

D_MODEL = 1024
BATCH = 4
SEQ = 4096
DEPTH = 1
DEC_BATCH = 128
DEC_SEQ = 1
PAST_LEN = 8192
PAGE_SIZE = 128

HEAD_DIM = 64
N_ATTN_HEADS = 16
D_ATTN = N_ATTN_HEADS * HEAD_DIM
DILATED_CONFIGS = ((128, 1), (512, 4), (2048, 16))
MAX_WINDOW = 2048
ATTN_BLOCK = 128
ROPE_THETA = 10000.0
NEG_INF = -1e30
N_SSM_HEADS = 16
SSM_HEAD_DIM = 64
D_SSM = N_SSM_HEADS * SSM_HEAD_DIM
SSM_GROUPS = 4
D_STATE = 128
CONV_WIDTH = 4
D_CONV = D_SSM + 2 * SSM_GROUPS * D_STATE
SSD_CHUNK = 128
D_MIX = D_ATTN + D_SSM
D_IN = 3 * D_ATTN + D_SSM + D_CONV + N_SSM_HEADS
SPLIT_POINTS = [D_ATTN, 2 * D_ATTN, 3 * D_ATTN, 3 * D_ATTN + D_SSM, 3 * D_ATTN + D_SSM + D_CONV]
D_FF = 4 * D_MODEL
N_MOD = 6
EPS = 1e-6

kernel_name = 'hybrid_dilated_attn_ssd_step'


def rms_norm(x):
    xf = x.astype(jnp.float32)
    return (xf * lax.rsqrt(jnp.mean(xf * xf, axis=-1, keepdims=True) + EPS)).astype(x.dtype)


def modulate(x, shift, scale):
    return rms_norm(x) * (1.0 + scale) + shift


def rope(t, pos):
    half = HEAD_DIM // 2
    inv_freq = ROPE_THETA ** (-jnp.arange(half, dtype=jnp.float32) / half)
    ang = pos.astype(jnp.float32)[:, None] * inv_freq[None, :]
    cos = jnp.cos(ang)[None, :, None, :]
    sin = jnp.sin(ang)[None, :, None, :]
    tf = t.astype(jnp.float32)
    t1, t2 = tf[..., :half], tf[..., half:]
    return jnp.concatenate([t1 * cos - t2 * sin, t1 * sin + t2 * cos], axis=-1).astype(t.dtype)


def dilated_band_attention(q, k, v, dil, n_steps):
    b, S, H, Dh = q.shape
    span = dil * ATTN_BLOCK
    s_pad = -(-S // span) * span
    L = s_pad // dil
    nb = L // ATTN_BLOCK

    def to_blocks(t):
        t = jnp.pad(t, ((0, 0), (0, s_pad - S), (0, 0), (0, 0)))
        t = t.reshape(b, L, dil, H, Dh).transpose(0, 2, 1, 3, 4)
        return t.reshape(b, dil, nb, ATTN_BLOCK, H, Dh)

    def with_prev(t):
        prev = jnp.pad(t, ((0, 0), (0, 0), (1, 0), (0, 0), (0, 0), (0, 0)))[:, :, :-1]
        return jnp.concatenate([prev, t], axis=3)

    qb = to_blocks(q)
    kc = with_prev(to_blocks(k))
    vc = with_prev(to_blocks(v))
    s = jnp.einsum('brnqhd,brnkhd->brnhqk', qb, kc).astype(jnp.float32)
    qi = jnp.arange(ATTN_BLOCK)[:, None] + ATTN_BLOCK
    kj = jnp.arange(2 * ATTN_BLOCK)[None, :]
    dist = qi - kj
    blk = jnp.arange(nb)[:, None, None]
    valid = (dist >= 0) & (dist <= n_steps) & ((blk > 0) | (kj >= ATTN_BLOCK))
    s = jnp.where(valid[None, None, :, None], s, NEG_INF)
    m = jnp.max(s, axis=-1, keepdims=True)
    p = jnp.exp(s - m)
    l = jnp.sum(p, axis=-1, keepdims=True)
    o = jnp.einsum('brnhqk,brnkhd->brnqhd', p / l, vc.astype(jnp.float32))
    lse = (m + jnp.log(l))[..., 0]
    o = o.reshape(b, dil, L, H, Dh).transpose(0, 2, 1, 3, 4).reshape(b, s_pad, H, Dh)[:, :S]
    lse = lse.transpose(0, 1, 2, 4, 3).reshape(b, dil, L, H).transpose(0, 2, 1, 3).reshape(b, s_pad, H)[:, :S]
    return o, lse


def dilated_cached_attention(q, k_all, v_all, dil, n_steps):
    T = q.shape[1]
    offset = k_all.shape[1] - T
    idx = (offset + jnp.arange(T))[:, None] - dil * jnp.arange(n_steps + 1)[None, :]
    valid = idx >= 0
    idx = jnp.clip(idx, 0)
    kg = jnp.take(k_all, idx, axis=1)
    vg = jnp.take(v_all, idx, axis=1)
    s = jnp.einsum('bthd,btjhd->bthj', q, kg).astype(jnp.float32)
    s = jnp.where(valid[None, :, None, :], s, NEG_INF)
    m = jnp.max(s, axis=-1, keepdims=True)
    p = jnp.exp(s - m)
    l = jnp.sum(p, axis=-1, keepdims=True)
    o = jnp.einsum('bthj,btjhd->bthd', p / l, vg.astype(jnp.float32))
    return o, (m + jnp.log(l))[..., 0]


def causal_dwconv(x_ext, conv_w, conv_b):
    y = lax.conv_general_dilated(x_ext, conv_w[:, None, :].astype(x_ext.dtype), window_strides=(1,), padding='VALID',
                                 dimension_numbers=('NWC', 'WIO', 'NWC'), feature_group_count=x_ext.shape[-1])
    return jax.nn.silu(y + conv_b)


def ssd_chunked(xs, dt, A, Bm, Cm, h0):
    b, S, G, Hg, P = xs.shape
    N = Bm.shape[-1]
    nc = S // SSD_CHUNK
    xc = xs.reshape(b, nc, SSD_CHUNK, G, Hg, P)
    dtc = dt.reshape(b, nc, SSD_CHUNK, G, Hg)
    Bc = Bm.reshape(b, nc, SSD_CHUNK, G, N)
    Cc = Cm.reshape(b, nc, SSD_CHUNK, G, N)
    cum = jnp.cumsum(dtc * A, axis=2)
    seg = cum[:, :, :, None] - cum[:, :, None, :]
    tril = jnp.tril(jnp.ones((SSD_CHUNK, SSD_CHUNK), bool))[:, :, None, None]
    decay = jnp.exp(jnp.where(tril, seg, -jnp.inf))
    y_intra = jnp.einsum('bclgn,bcsgn,bclsgh,bcsgh,bcsghp->bclghp', Cc, Bc, decay, dtc, xc)
    decay_to_end = jnp.exp(cum[:, :, -1:] - cum)
    chunk_state = jnp.einsum('bclgh,bclgh,bclghp,bclgn->bcghpn', decay_to_end, dtc, xc, Bc)
    chunk_decay = jnp.exp(cum[:, :, -1])

    def step(h, inp):
        cs, cd = inp
        return h * cd[..., None, None] + cs, h

    h_last, h_starts = lax.scan(step, h0, (jnp.moveaxis(chunk_state, 1, 0), jnp.moveaxis(chunk_decay, 1, 0)))
    h_starts = jnp.moveaxis(h_starts, 0, 1)
    y_inter = jnp.einsum('bclgn,bcghpn,bclgh->bclghp', Cc, h_starts, jnp.exp(cum))
    return (y_intra + y_inter).reshape(b, S, G, Hg, P), h_last


def ssd_recurrent(xs, dt, A, Bm, Cm, h0):
    def step(h, inp):
        x_t, dt_t, b_t, c_t = inp
        h = h * jnp.exp(dt_t * A)[..., None, None] + jnp.einsum('bgh,bghp,bgn->bghpn', dt_t, x_t, b_t)
        return h, jnp.einsum('bgn,bghpn->bghp', c_t, h)

    h_last, ys = lax.scan(step, h0, (jnp.moveaxis(xs, 1, 0), jnp.moveaxis(dt, 1, 0), jnp.moveaxis(Bm, 1, 0), jnp.moveaxis(Cm, 1, 0)))
    return jnp.moveaxis(ys, 0, 1), h_last


def hybrid_layer(x, c, pos, cache_k, cache_v, conv_buf, ssm_h, w_ada, b_ada, w_in, conv_w, conv_b,
                 dt_bias, a_log, d_skip, g_attn, g_ssm, w_out, w_up, w_down):
    fresh = cache_k is None
    b, T, _ = x.shape
    G, Hg = SSM_GROUPS, N_SSM_HEADS // SSM_GROUPS
    sh1, sc1, g1, sh2, sc2, g2 = jnp.split((jax.nn.silu(c) @ w_ada + b_ada)[:, None, :], N_MOD, axis=-1)

    h = modulate(x, sh1, sc1)
    q, k, v, z, xbc, dt_raw = jnp.split(h @ w_in, SPLIT_POINTS, axis=-1)
    q = rope(q.reshape(b, T, N_ATTN_HEADS, HEAD_DIM), pos) * (HEAD_DIM ** -0.5)
    k = rope(k.reshape(b, T, N_ATTN_HEADS, HEAD_DIM), pos)
    v = v.reshape(b, T, N_ATTN_HEADS, HEAD_DIM)
    if fresh:
        k_all, v_all = k, v
        parts = [dilated_band_attention(q, k, v, d, w // d) for w, d in DILATED_CONFIGS]
        xbc_ext = jnp.pad(xbc, ((0, 0), (CONV_WIDTH - 1, 0), (0, 0)))
        h0 = jnp.zeros((b, G, Hg, SSM_HEAD_DIM, D_STATE), jnp.float32)
    else:
        k_all = jnp.concatenate([cache_k.astype(k.dtype), k], axis=1)
        v_all = jnp.concatenate([cache_v.astype(v.dtype), v], axis=1)
        parts = [dilated_cached_attention(q, k_all, v_all, d, w // d) for w, d in DILATED_CONFIGS]
        xbc_ext = jnp.concatenate([conv_buf.astype(xbc.dtype), xbc], axis=1)
        h0 = ssm_h.astype(jnp.float32).reshape(b, G, Hg, SSM_HEAD_DIM, D_STATE)
    o_parts = jnp.stack([o for o, _ in parts])
    wts = jax.nn.softmax(jnp.stack([l for _, l in parts]), axis=0)
    attn = jnp.einsum('cbthd,cbth->bthd', o_parts, wts).reshape(b, T, D_ATTN).astype(x.dtype)

    xbc_c = causal_dwconv(xbc_ext, conv_w, conv_b)
    xs, Bm, Cm = jnp.split(xbc_c.astype(jnp.float32), [D_SSM, D_SSM + G * D_STATE], axis=-1)
    xs = xs.reshape(b, T, G, Hg, SSM_HEAD_DIM)
    Bm = Bm.reshape(b, T, G, D_STATE)
    Cm = Cm.reshape(b, T, G, D_STATE)
    dt = jax.nn.softplus(dt_raw.astype(jnp.float32) + dt_bias.astype(jnp.float32)).reshape(b, T, G, Hg)
    A = -jnp.exp(a_log.astype(jnp.float32)).reshape(G, Hg)
    if fresh:
        y, h_last = ssd_chunked(xs, dt, A, Bm, Cm, h0)
    else:
        y, h_last = ssd_recurrent(xs, dt, A, Bm, Cm, h0)
    y = y + d_skip.astype(jnp.float32).reshape(G, Hg)[:, :, None] * xs
    y = y.reshape(b, T, D_SSM) * jax.nn.silu(z.astype(jnp.float32))
    y = rms_norm(y.reshape(b, T, G, D_SSM // G)).reshape(b, T, D_SSM) * g_ssm

    mix = jnp.concatenate([rms_norm(attn) * g_attn, y.astype(x.dtype)], axis=-1) @ w_out
    x = x + g1 * mix

    u = jnp.square(jax.nn.relu(modulate(x, sh2, sc2) @ w_up))
    x = x + g2 * (u @ w_down)

    win = min(MAX_WINDOW, k_all.shape[1])
    return (x, k_all[:, -win:], v_all[:, -win:], xbc_ext[:, -(CONV_WIDTH - 1):],
            h_last.reshape(b, N_SSM_HEADS, SSM_HEAD_DIM, D_STATE))


def setup_inputs(seed: int = 0) -> dict:
    key = jax.random.key(seed)
    ks = jax.random.split(key, 24)
    f32 = jnp.float32
    win_buf = min(MAX_WINDOW, PAST_LEN)

    def nrm(k, shape, scale):
        return jax.random.normal(k, shape, f32) * scale

    dt0 = jnp.exp(jax.random.uniform(ks[12], (DEPTH, N_SSM_HEADS), f32, math.log(1e-3), math.log(1e-1)))
    return {
        'x_prompt': nrm(ks[0], (BATCH, SEQ, D_MODEL), 1.0),
        'x_sample': nrm(ks[1], (DEC_BATCH, DEC_SEQ, D_MODEL), 1.0),
        'c_prompt': nrm(ks[2], (BATCH, D_MODEL), 1.0),
        'c_sample': nrm(ks[3], (DEC_BATCH, D_MODEL), 1.0),
        'cache_k_win': nrm(ks[4], (DEPTH, DEC_BATCH, win_buf, N_ATTN_HEADS, HEAD_DIM), 1.0),
        'cache_v_win': nrm(ks[5], (DEPTH, DEC_BATCH, win_buf, N_ATTN_HEADS, HEAD_DIM), 1.0),
        'state_conv': nrm(ks[6], (DEPTH, DEC_BATCH, CONV_WIDTH - 1, D_CONV), 1.0),
        'state_ssm': nrm(ks[7], (DEPTH, DEC_BATCH, N_SSM_HEADS, SSM_HEAD_DIM, D_STATE), 0.1),
        'w_ada': nrm(ks[8], (DEPTH, D_MODEL, N_MOD * D_MODEL), 0.5 * D_MODEL ** -0.5),
        'b_ada': nrm(ks[9], (DEPTH, N_MOD * D_MODEL), 0.02),
        'w_in': nrm(ks[10], (DEPTH, D_MODEL, D_IN), D_MODEL ** -0.5),
        'conv_w': nrm(ks[11], (DEPTH, CONV_WIDTH, D_CONV), CONV_WIDTH ** -0.5),
        'conv_b': nrm(ks[13], (DEPTH, D_CONV), 0.02),
        'dt_bias': dt0 + jnp.log(-jnp.expm1(-dt0)),
        'a_log': jnp.log(jax.random.uniform(ks[14], (DEPTH, N_SSM_HEADS), f32, 1.0, 16.0)),
        'd_skip': 1.0 + nrm(ks[15], (DEPTH, N_SSM_HEADS), 0.1),
        'g_attn': 1.0 + nrm(ks[16], (DEPTH, D_ATTN), 0.1),
        'g_ssm': 1.0 + nrm(ks[17], (DEPTH, D_SSM), 0.1),
        'w_out': nrm(ks[18], (DEPTH, D_MIX, D_MODEL), D_MIX ** -0.5),
        'w_up': nrm(ks[19], (DEPTH, D_MODEL, D_FF), D_MODEL ** -0.5),
        'w_down': nrm(ks[20], (DEPTH, D_FF, D_MODEL), D_FF ** -0.5),
        'g_final': 1.0 + nrm(ks[21], (D_MODEL,), 0.1),
    }


def reference(x_prompt, x_sample, c_prompt, c_sample, cache_k_win, cache_v_win, state_conv, state_ssm,
              w_ada, b_ada, w_in, conv_w, conv_b, dt_bias, a_log, d_skip, g_attn, g_ssm, w_out, w_up, w_down, g_final):
    pos_p = jnp.arange(x_prompt.shape[1], dtype=jnp.int32)
    pos_s = PAST_LEN + jnp.arange(x_sample.shape[1], dtype=jnp.int32)
    xp, xs = x_prompt, x_sample
    kp_l, vp_l, cp_l, hp_l, ks_l, vs_l, cs_l, hs_l = [], [], [], [], [], [], [], []
    for layer in range(DEPTH):
        lw = (w_ada[layer], b_ada[layer], w_in[layer], conv_w[layer], conv_b[layer], dt_bias[layer], a_log[layer],
              d_skip[layer], g_attn[layer], g_ssm[layer], w_out[layer], w_up[layer], w_down[layer])
        xp, kp, vp, cp, hp = hybrid_layer(xp, c_prompt, pos_p, None, None, None, None, *lw)
        xs, kn, vn, cn, hn = hybrid_layer(xs, c_sample, pos_s, cache_k_win[layer], cache_v_win[layer],
                                          state_conv[layer], state_ssm[layer], *lw)
        kp_l.append(kp); vp_l.append(vp); cp_l.append(cp); hp_l.append(hp)
        ks_l.append(kn); vs_l.append(vn); cs_l.append(cn); hs_l.append(hn)
    y_prompt = rms_norm(xp) * g_final
    y_sample = rms_norm(xs) * g_final
    return (y_prompt, y_sample, jnp.stack(kp_l), jnp.stack(vp_l), jnp.stack(cp_l), jnp.stack(hp_l),
            jnp.stack(ks_l), jnp.stack(vs_l), jnp.stack(cs_l), jnp.stack(hs_l))
```

```python
from contextlib import ExitStack

import numpy as np
import concourse.bass as bass
import concourse.mybir as mybir
from concourse.bass_utils import run_bass_kernel_spmd

F32 = mybir.dt.float32
BF16 = mybir.dt.bfloat16
AF = mybir.ActivationFunctionType
ALU = mybir.AluOpType
AX = mybir.AxisListType

NCORES = 8
D = 1024
S_OWN = 2048
NTOK = 4096
NS = 16
DIN = 6160
EPS = 1e-6
NEG = -30000.0
CONFIGS = ((128, 1), (512, 4), (2048, 16))


def ssl(start, n, step):
    return slice(start, start + (n - 1) * step + 1, step)


class Buf:
    __slots__ = ("name", "w", "r", "dsem", "dcnt")

    def __init__(self, name):
        self.name = name
        self.w = None
        self.r = {}
        self.dsem = None
        self.dcnt = 0


class FW:
    QUEUES = ("pe", "dve", "act", "pool", "sp")

    def __init__(self, nc, es):
        self.nc = nc
        self.es = es
        self.sems = {}
        self.cnt = {q: 0 for q in self.QUEUES}
        self.seen = {q: {} for q in self.QUEUES}
        self.ops = {q: [] for q in self.QUEUES}
        self.pending = {q: {} for q in self.QUEUES}
        for q in self.QUEUES:
            self._sem("q_" + q)
        self.nbuf = 0
        self.dbufs = []
        self.final_events = []
        self.enabled = True

    def _sem(self, key):
        if key not in self.sems:
            self.sems[key] = self.es.enter_context(self.nc.semaphore(key))
        return self.sems[key]

    def buf(self, name=None):
        self.nbuf += 1
        return Buf(name or f"b{self.nbuf}")

    def _need(self, q, ev, waits):
        if ev is None:
            return
        key, val = ev
        if self.seen[q].get(key, 0) >= val:
            return
        self.seen[q][key] = val
        waits[key] = max(waits.get(key, 0), val)

    def _deps(self, q, reads, writes, extra):
        waits = {}
        for k, v in self.pending[q].items():
            self._need(q, (k, v), waits)
        self.pending[q] = {}
        for b in reads:
            self._need(q, b.w, waits)
        for b in writes:
            self._need(q, b.w, waits)
            for k, v in b.r.items():
                self._need(q, (k, v), waits)
        for ev in extra:
            self._need(q, ev, waits)
        return waits

    def _commit(self, ev, reads, writes):
        k, v = ev
        for b in reads:
            b.r[k] = max(b.r.get(k, 0), v)
        for b in writes:
            b.w = ev
            b.r = {}

    def op(self, q, fn, reads=(), writes=(), extra=()):
        if not self.enabled:
            return None
        reads = [b for b in reads if b is not None]
        writes = [b for b in writes if b is not None]
        waits = self._deps(q, reads, writes, extra)
        if q == "pe":
            waits.pop("q_pe", None)
        self.cnt[q] += 1
        ev = ("q_" + q, self.cnt[q])
        self.ops[q].append((waits, fn, ev[0], 1))
        self._commit(ev, reads, writes)
        return ev

    def dma(self, q, fn, reads=(), writes=(), track=None, extra=()):
        if not self.enabled:
            return None
        reads = [b for b in reads if b is not None]
        writes = [b for b in writes if b is not None]
        waits = self._deps(q, reads, writes, extra)
        tb = track or (writes[0] if writes else reads[0])
        if tb.dsem is None:
            self.nbuf += 1
            tb.dsem = f"d{self.nbuf}_{tb.name}"
            self._sem(tb.dsem)
            self.dbufs.append(tb)
        tb.dcnt += 16
        ev = (tb.dsem, tb.dcnt)
        self.ops[q].append((waits, fn, ev[0], 16))
        self._commit(ev, reads, writes)
        return ev

    def barrier(self):
        allw = {"q_" + q: self.cnt[q] for q in self.QUEUES if q != "sp" and self.cnt[q] > 0}
        for b in self.dbufs:
            if b.name not in ("kvcopy", "convcopy"):
                allw[b.dsem] = b.dcnt
        for q in self.QUEUES:
            for k, v in allw.items():
                if k == "q_" + q:
                    continue
                self.pending[q][k] = max(self.pending[q].get(k, 0), v)

    def emit(self):
        nc = self.nc
        fwt = {}
        for b in self.dbufs:
            fwt[b.dsem] = b.dcnt
        for q in self.QUEUES:
            if q != "sp" and self.cnt[q] > 0:
                fwt["q_" + q] = self.cnt[q]
        sems = self.sems
        ops = self.ops

        def run(q, eng):
            for waits, fn, skey, inc in ops[q]:
                for k, v in waits.items():
                    eng.wait_ge(sems[k], v)
                fn(eng).then_inc(sems[skey], inc)
            if q == "sp":
                for k, v in fwt.items():
                    eng.wait_ge(sems[k], v)

        with nc.Block() as block:
            @block.tensor
            def _(e):
                run("pe", e)

            @block.vector
            def _(e):
                run("dve", e)

            @block.scalar
            def _(e):
                run("act", e)

            @block.gpsimd
            def _(e):
                run("pool", e)

            @block.sync
            def _(e):
                run("sp", e)


def build_program(phases=None):
    nc = bass.Bass("TRN2", target_bir_lowering=False)

    def din(name, shape, dt=F32):
        return nc.dram_tensor(name, list(shape), dt, kind="ExternalInput").ap()

    def dout(name, shape, dt=F32):
        return nc.dram_tensor(name, list(shape), dt, kind="ExternalOutput").ap()

    def dscr(name, shape, dt=F32):
        return nc.dram_tensor(name, list(shape), dt).ap()

    x_tok = din("x_tok", [NTOK, D])
    x_s = din("x_s", [NS, D])
    c_pT = din("c_pT", [128, 8])
    c_sT = din("c_sT", [128, 8, NS])
    cache_k = din("cache_k", [NS, 2048, D])
    cache_v = din("cache_v", [NS, 2048, D])
    st_conv = din("st_conv", [NS, 3, 2048])
    st_ssm = din("st_ssm", [NS, 1024, 128])
    w_ada = din("w_ada", [D, 6 * D])
    b_ada = din("b_ada", [1, 6 * D])
    w_in = din("w_in", [D, DIN])
    conv_wT = din("conv_wT", [128, 16, 4])
    conv_bT = din("conv_bT", [128, 16])
    conv_w = din("conv_w", [4, 2048])
    conv_b = din("conv_b", [1, 2048])
    dt_bias = din("dt_bias", [1, 16])
    a_log = din("a_log", [1, 16])
    d_skip = din("d_skip", [1, 16])
    g_col = din("g_col", [128, 16])
    w_out = din("w_out", [2 * D, D])
    w_up = din("w_up", [D, 4 * D])
    w_down = din("w_down", [4 * D, D])
    g_final = din("g_final", [1, D])
    k_ident = din("k_ident", [128, 128])
    k_tri = din("k_tri", [128, 128])
    k_maskr = din("k_maskr", [128, 256])
    k_maske = din("k_maske", [128, 256])
    k_rope = din("k_rope", [128, 33, 2, 64])
    k_mneg = din("k_mneg", [128, 128])
    k_sel = din("k_sel", [16, 16, 128])
    k_halo = din("k_halo", [128, 1])

    y_own = dout("y_own", [S_OWN, D])
    y_s = dout("y_s", [NS, D])
    kp = dout("kp", [S_OWN, D])
    vp = dout("vp", [S_OWN, D])
    convp = dout("convp", [3, 2048])
    ssmp = dout("ssmp", [1024, 128])
    ks = dout("ks", [NS, 2048, D])
    vs = dout("vs", [NS, 2048, D])
    convs = dout("convs", [NS, 3, 2048])
    ssms = dout("ssms", [NS, 1024, 128])

    modp_d = dscr("modp_d", [128, 6 * D])
    mods_d = dscr("mods_d", [NS, 6 * D])
    onat_d = dscr("onat_d", [3, S_OWN, 132])
    mixT_d = dscr("mixT_d", [16, 128, S_OWN], BF16)
    x1_d = dscr("x1_d", [S_OWN, D])
    h2T_d = dscr("h2T_d", [8, 128, S_OWN], BF16)
    qkvs_d = dscr("qkvs_d", [NS, 3, D])
    zxds_d = dscr("zxds_d", [NS, 3088])
    mixs_d = dscr("mixs_d", [NS, 2 * D])
    x1s_d = dscr("x1s_d", [NS, D])

    w_in_k = w_in.rearrange("(kc k) c -> k kc c", k=128)

    with ExitStack() as es:
        fw = FW(nc, es)

        def T(st, name, shape, dt):
            return st.enter_context(nc.sbuf_tensor(name, list(shape), dt)), fw.buf(name)

        def PT(st, name, shape, dt):
            return st.enter_context(nc.psum_tensor(name, list(shape), dt)), fw.buf(name)

        def mm(o, l, r, st, sp, rd, wr):
            fw.op("pe", lambda e: e.matmul(o, lhsT=l, rhs=r, start=st, stop=sp), rd, wr)

        def tr(o, i, idn, rd, wr):
            fw.op("pe", lambda e: e.transpose(o, i, idn), rd, wr)

        def act(o, i, f, rd, wr, bias=None, scale=None, accum=None):
            kw = {}
            if bias is not None:
                kw["bias"] = bias
            if scale is not None:
                kw["scale"] = scale
            if accum is not None:
                kw["accum_out"] = accum
            fw.op("act", lambda e: e.activation(out=o, in_=i, func=f, **kw), rd, wr)

        def tt(q, o, a, b, op, rd, wr):
            fw.op(q, lambda e: e.tensor_tensor(out=o, in0=a, in1=b, op=op), rd, wr)

        def ts(q, o, a, s1, s2, op0, op1, rd, wr):
            if s2 is None:
                fw.op(q, lambda e: e.tensor_scalar(out=o, in0=a, scalar1=s1, scalar2=None, op0=op0), rd, wr)
            else:
                fw.op(q, lambda e: e.tensor_scalar(out=o, in0=a, scalar1=s1, scalar2=s2, op0=op0, op1=op1), rd, wr)

        def stt(o, a, s, b, op0, op1, rd, wr):
            fw.op("dve", lambda e: e.scalar_tensor_tensor(out=o, in0=a, scalar=s, in1=b, op0=op0, op1=op1), rd, wr)

        def cp(q, o, i, rd, wr):
            if q == "act":
                fw.op("act", lambda e: e.copy(out=o, in_=i), rd, wr)
            else:
                fw.op(q, lambda e: e.tensor_copy(out=o, in_=i), rd, wr)

        def red(o, i, op, rd, wr, negate=False):
            fw.op("dve", lambda e: e.tensor_reduce(out=o, in_=i, axis=AX.X, op=op, negate=negate), rd, wr)

        def rcp(o, i, rd, wr):
            fw.op("dve", lambda e: e.reciprocal(out=o, in_=i), rd, wr)

        def mset(q, o, v, wr):
            fw.op(q, lambda e: e.memset(o, v), (), wr)

        def dma(q, o, i, rd, wr, track=None, extra=(), slow=False):
            if slow:
                return fw.dma(q, lambda e: e.dma_start(out=o, in_=i, allow_slow_non_contiguous=True), rd, wr, track, extra)
            return fw.dma(q, lambda e: e.dma_start(out=o, in_=i), rd, wr, track, extra)

        b_in = fw.buf("inputs")

        def phase(name):
            fw.enabled = phases is None or name in phases

        def sub(name, x):
            if phases is None:
                fw.enabled = True
            elif name not in phases:
                fw.enabled = False
            else:
                subs = [p_ for p_ in phases if p_.startswith(name) and p_ != name]
                fw.enabled = (not subs) or (name + x in subs)
        NP_ = 8 if phases is None else int(next((p_[2:] for p_ in phases if p_.startswith("np")), 8))

        def rmsnorm_stats(np_, xin, bx, ssq, junk, bj, bs, width):
            act(junk[:np_, :width], xin, AF.Square, [bx], [bj, bs], accum=ssq[:np_, 0:1])
            ts("dve", ssq[:np_, 1:2], ssq[:np_, 0:1], 1.0 / width, EPS, ALU.mult, ALU.add, [bs], [bs])
            act(ssq[:np_, 2:3], ssq[:np_, 1:2], AF.Sqrt, [bs], [bs])
            rcp(ssq[:np_, 3:4], ssq[:np_, 2:3], [bs], [bs])

        G = es
        ident_f, b_identf = T(G, "ident_f", [128, 128], F32)
        ident_b, b_identb = T(G, "ident_b", [128, 128], BF16)
        ones_f, b_ones = T(G, "ones_f", [128, 128], F32)
        halo, b_halo = T(G, "halo", [128, 1], F32)
        hsT, b_hsT = T(G, "hsT", [128, 8, NS], BF16)
        ssq_at, b_ssqat = T(G, "ssq_at", [128, 16], F32)
        sel, b_sel = T(G, "sel", [16, 16, 128], F32)
        h2sT, b_h2sT = T(G, "h2sT", [128, 8, NS], BF16)

        ps0, b_ps0 = PT(G, "ps0", [128, 512], F32)
        ps1, b_ps1 = PT(G, "ps1", [128, 512], F32)
        ps23, b_ps23 = PT(G, "ps23", [128, 1024], F32)
        ps45, b_ps45 = PT(G, "ps45", [128, 1024], F32)
        ps6, b_ps6 = PT(G, "ps6", [128, 512], F32)
        pst, b_pst = PT(G, "pst", [128, 1024], BF16)

        dma("sp", ident_f[:], k_ident, [b_in], [b_identf])
        dma("pool", ident_b[:], k_ident, [b_in], [b_identb])
        dma("sp", halo[:], k_halo, [b_in], [b_halo])
        dma("sp", sel[:], k_sel, [b_in], [b_sel])
        mset("dve", ones_f[:], 1.0, [b_ones])
        mset("dve", ssq_at[:], 0.0, [b_ssqat])

        phase("kv")
        b_kv = fw.buf("kvcopy")

        def kv_copy(s_):
            dma("sp", ks[s_, 0:2047, :], cache_k[s_, 1:2048, :], [b_in], [], track=b_kv)
            dma("sp", vs[s_, 0:2047, :], cache_v[s_, 1:2048, :], [b_in], [], track=b_kv)
        b_cv = fw.buf("convcopy")
        dma("sp", convs[:, 0:2, :], st_conv[:, 1:3, :], [b_in], [], track=b_cv)

        phase("p0")
        with ExitStack() as ph:
            cps, b_cps = T(ph, "cps", [128, 8], F32)
            css, b_css = T(ph, "css", [128, 8, NS], F32)
            lhsP, b_lhsP = T(ph, "lhsP", [128, 8, 128], F32)
            wa, b_wa = T(ph, "wa", [128, 8, 512], F32)
            bab, b_bab = T(ph, "bab", [128, 512], F32)
            mo, b_mo = T(ph, "mo", [128, 512], F32)
            mo2, b_mo2 = T(ph, "mo2", [NS, 512], F32)
            dma("sp", cps[:], c_pT, [b_in], [b_cps])
            dma("sp", css[:], c_sT, [b_in], [b_css])
            act(cps[:], cps[:], AF.Silu, [b_cps], [b_cps])
            act(css[:], css[:], AF.Silu, [b_css], [b_css])
            cp("dve", lhsP[:], cps[:].unsqueeze(2).broadcast_to([128, 8, 128]), [b_cps], [b_lhsP])
            w_ada_k = w_ada.rearrange("(kc k) c -> k kc c", k=128)
            for g in range(12):
                cs_ = slice(g * 512, (g + 1) * 512)
                dma("sp", wa[:], w_ada_k[:, :, cs_], [b_in], [b_wa])
                dma("sp", bab[:], b_ada[0:1, cs_].partition_broadcast(128), [b_in], [b_bab])
                for kc in range(8):
                    mm(ps0[:, :], lhsP[:, kc, :], wa[:, kc, :], kc == 0, kc == 7, [b_lhsP, b_wa], [b_ps0])
                for kc in range(8):
                    mm(ps1[0:NS, :], css[:, kc, :], wa[:, kc, :], kc == 0, kc == 7, [b_css, b_wa], [b_ps1])
                add1 = 1.0 if (g // 2) in (1, 4) else 0.0
                stt(mo[:], ps0[:, :], add1, bab[:], ALU.add, ALU.add, [b_ps0, b_bab], [b_mo])
                stt(mo2[:], ps1[0:NS, :], add1, bab[0:NS, :], ALU.add, ALU.add, [b_ps1, b_bab], [b_mo2])
                dma("sp", modp_d[:, cs_], mo[:], [b_mo], [], track=b_mo)
                dma("sp", mods_d[:, cs_], mo2[:], [b_mo2], [], track=b_mo2)
        fw.barrier()

        with ExitStack() as phA:
            hT, b_hT = T(phA, "hT", [128, 8, NTOK], BF16)

            phase("a0")
            with ExitStack() as ph:
                sc1, b_sc1 = T(ph, "sc1", [128, D], F32)
                sh1, b_sh1 = T(ph, "sh1", [128, D], F32)
                xt, b_xt = T(ph, "xt", [128, D], F32)
                junk, b_junk = T(ph, "junk", [128, D], BF16)
                tmp, b_tmp = T(ph, "tmpn", [128, D], F32)
                hb, b_hb = T(ph, "hb", [128, D], BF16)
                st4, b_st4 = T(ph, "st4", [128, 4], F32)
                dma("sp", sh1[:], modp_d[:, 0:D], [], [b_sh1])
                dma("sp", sc1[:], modp_d[:, D:2 * D], [], [b_sc1])
                for t in range(33):
                    npar = 128 if t < 32 else NS
                    if t < 32:
                        dma("sp", xt[:], x_tok[t * 128:(t + 1) * 128, :], [b_in], [b_xt])
                    else:
                        dma("sp", xt[0:NS, :], x_s, [b_in], [b_xt])
                        dma("sp", sh1[0:NS, :], mods_d[:, 0:D], [], [b_sh1])
                        dma("sp", sc1[0:NS, :], mods_d[:, D:2 * D], [], [b_sc1])
                    rmsnorm_stats(npar, xt[:npar, :], b_xt, st4, junk, b_junk, b_st4, D)
                    stt(tmp[:npar, :], xt[:npar, :], st4[:npar, 3:4], sc1[:npar, :], ALU.mult, ALU.mult,
                        [b_xt, b_st4, b_sc1], [b_tmp])
                    tt("dve", hb[:npar, :], tmp[:npar, :], sh1[:npar, :], ALU.add, [b_tmp, b_sh1], [b_hb])
                    for kc in range(8):
                        tr(pst[:, kc * 128:kc * 128 + npar], hb[:npar, kc * 128:(kc + 1) * 128],
                           ident_b[:npar, :npar], [b_hb, b_identb], [b_pst])
                    if t < 32:
                        cp("act", hT[:, :, t * 128:(t + 1) * 128],
                           pst[:, :].rearrange("p (a b) -> p a b", a=8), [b_pst], [b_hT])
                    else:
                        cp("act", hsT[:, :, :], pst[:, :].rearrange("p (a b) -> p a b", a=8)[:, :, 0:NS],
                           [b_pst], [b_hsT])
            fw.barrier()

            phase("a2")
            with ExitStack() as ph:
                maskr, b_maskr = T(ph, "maskr", [128, 256], F32)
                maske, b_maske = T(ph, "maske", [128, 256], F32)
                rope, b_rope = T(ph, "rope", [128, 33, 2, 64], F32)
                wqkv, b_wqkv = T(ph, "wqkv", [128, 8, 3, 128], BF16)
                KT, b_KT = T(ph, "KT", [128, NTOK], BF16)
                QT, b_QT = T(ph, "QT", [128, 2, S_OWN], BF16)
                Vn, b_Vn = T(ph, "Vn", [128, 17, 128], BF16)
                V4, b_V4 = T(ph, "V4", [128, 20, 128], BF16)
                V16, b_V16 = T(ph, "V16", [128, 32, 128], BF16)
                rk, b_rk = T(ph, "rk", [128, 4, 64], F32)
                ra, b_ra = T(ph, "ra", [128, 4, 64], F32)
                rb_, b_rb = T(ph, "rb", [128, 4, 64], F32)
                rkb, b_rkb = T(ph, "rkb", [128, 4, 64], BF16)
                vf, b_vf = T(ph, "vf", [128, 128], F32)
                sm, b_sm = T(ph, "sm", [128, 2, 256], F32)
                pb, b_pb = T(ph, "pb", [128, 2, 256], BF16)
                ptb, b_ptb = T(ph, "ptb", [128, 4, 128], BF16)
                stt_, b_stt = T(ph, "stt", [128, 12], F32)
                stages = [T(ph, f"stage{i}", [128, 132], F32) for i in range(4)]
                onat = [T(ph, f"onat{c}", [128, 16, 132], F32) for c in range(3)]
                mg, b_mg = T(ph, "mg", [128, 16, 8], F32)
                wgt, b_wgt = T(ph, "wgt", [128, 3, 16, 2], F32)
                attb, b_attb = T(ph, "attb", [128, 16, 128], BF16)
                attT, b_attT = T(ph, "attT", [128, S_OWN], BF16)
                sq16, b_sq16 = T(ph, "sq16", [128, 16], F32)
                qs3, b_qs3 = T(ph, "qs3", [NS, 3, 128], F32)
                mset("pool", QT[:], 0.0, [b_QT])
                dma("sp", maskr[:], k_maskr, [b_in], [b_maskr])
                dma("sp", maske[:], k_maske, [b_in], [b_maske])
                dma("sp", rope[:], k_rope, [b_in], [b_rope])
                b_onatd = fw.buf("onat_d")
                rd_ev = []
                for p in range(NP_):
                    sub("a2", "a")
                    for j in range(3):
                        dma("pool", wqkv[:, :, j, :], w_in_k[:, :, j * D + p * 128: j * D + (p + 1) * 128],
                            [b_in], [b_wqkv])
                    wflat = wqkv[:, :, :, :].rearrange("p k a b -> p k (a b)")
                    phase("kv")
                    kv_copy(2 * p)
                    kv_copy(2 * p + 1)
                    sub("a2", "a")
                    for t in range(33):
                        npar = 128 if t < 32 else NS
                        for kc in range(8):
                            lt = hT[:, kc, t * 128:(t + 1) * 128] if t < 32 else hsT[:, kc, :]
                            mm(ps0[:npar, 0:384], lt, wflat[:, kc, :], kc == 0, kc == 7,
                               [b_hT, b_hsT, b_wqkv], [b_ps0])
                        src = ps0[:npar, 0:256].rearrange("p (j d) -> p j d", j=4)
                        cosb = rope[:npar, t, 0, :].unsqueeze(1).broadcast_to([npar, 4, 64])
                        sinb = rope[:npar, t, 1, :].unsqueeze(1).broadcast_to([npar, 4, 64])
                        tt("dve", ra[:npar], src, cosb, ALU.mult, [b_ps0, b_rope], [b_ra])
                        tt("dve", rb_[:npar, :, 0:32], src[:, :, 32:64], sinb[:, :, 0:32], ALU.mult,
                           [b_ps0, b_rope], [b_rb])
                        tt("dve", rb_[:npar, :, 32:64], src[:, :, 0:32], sinb[:, :, 32:64], ALU.mult,
                           [b_ps0, b_rope], [b_rb])
                        tt("dve", rk[:npar], ra[:npar], rb_[:npar], ALU.add, [b_ra, b_rb], [b_rk])
                        if t == 32:
                            cp("dve", qs3[:, 0:2, :].rearrange("p a (h d) -> p (a h) d", h=2), rk[0:NS], [b_rk], [b_qs3])
                            cp("act", qs3[:, 2, :], ps0[0:NS, 256:384], [b_ps0], [b_qs3])
                            dma("sp", qkvs_d[:, :, p * 128:(p + 1) * 128], qs3[:], [b_qs3], [], track=b_qs3)
                            continue
                        cp("act", rkb[:], rk[:], [b_rk], [b_rkb])
                        tr(pst[:, 0:128], rkb[:, 2:4, :].rearrange("p a b -> p (a b)"), ident_b[:],
                           [b_rkb, b_identb], [b_pst])
                        if t >= 16:
                            tr(pst[:, 128:256], rkb[:, 0:2, :].rearrange("p a b -> p (a b)"), ident_b[:],
                               [b_rkb, b_identb], [b_pst])
                        cp("act", KT[:, t * 128:(t + 1) * 128], pst[:, 0:128], [b_pst], [b_KT])
                        if t >= 16:
                            cp("act", QT[0:64, 0, (t - 16) * 128:(t - 15) * 128], pst[0:64, 128:256], [b_pst], [b_QT])
                            cp("act", QT[64:128, 1, (t - 16) * 128:(t - 15) * 128], pst[64:128, 128:256], [b_pst], [b_QT])
                            dma("sp", kp[(t - 16) * 128:(t - 15) * 128, p * 128:(p + 1) * 128],
                                rk[:, 2:4, :].rearrange("p a b -> p (a b)"), [b_rk], [], track=b_rk)
                            cp("dve", vf[:], ps0[:, 256:384], [b_ps0], [b_vf])
                            dma("sp", vp[(t - 16) * 128:(t - 15) * 128, p * 128:(p + 1) * 128], vf[:],
                                [b_vf], [], track=b_vf)
                        if t >= 15:
                            cp("act", Vn[:, t - 15, :], ps0[:, 256:384], [b_ps0], [b_Vn])
                    sub("a2", "b")
                    for (Vd, b_Vd, dil, nreg, first) in ((V4, b_V4, 4, 5, 1536), (V16, b_V16, 16, 2, 0)):
                        span = 128 * dil
                        blk = 0
                        for rg in range(nreg):
                            for r in range(dil):
                                t0 = first + rg * span + r
                                col = (blk % 4) * 128
                                for kc in range(8):
                                    mm(ps1[:, col:col + 128], hT[:, kc, ssl(t0, 128, dil)], wqkv[:, kc, 2, :],
                                       kc == 0, kc == 7, [b_hT, b_wqkv], [b_ps1])
                                if blk % 4 == 3:
                                    cp("act", Vd[:, blk - 3:blk + 1, :],
                                       ps1[:, :].rearrange("p (a b) -> p a b", a=4), [b_ps1], [b_Vd])
                                blk += 1
                    sub("a2", "c")
                    wr_ev = []
                    nblk = 0
                    for c, (win, dil) in enumerate(CONFIGS):
                        span = 128 * dil
                        for R in range(S_OWN // span):
                            for r in range(dil):
                                oq = R * span + r
                                k0 = S_OWN + (R - 1) * span + r
                                for h in range(2):
                                    mm(ps6[:, h * 256:(h + 1) * 256],
                                       QT[:, h, ssl(oq, 128, dil)],
                                       KT[:, ssl(k0, 256, dil)],
                                       True, True, [b_QT, b_KT], [b_ps6])
                                mk, b_mk = (maske, b_maske) if R == 0 else (maskr, b_maskr)
                                tt("dve", sm[:], ps6[:, :].rearrange("p (a b) -> p a b", a=2),
                                   mk[:].unsqueeze(1).broadcast_to([128, 2, 256]), ALU.add,
                                   [b_ps6, b_mk], [b_sm])
                                red(stt_[:, 0:2], sm[:], ALU.max, [b_sm], [b_stt], negate=True)
                                ts("dve", stt_[:, 2:4], stt_[:, 0:2], 0.125, None, ALU.mult, None, [b_stt], [b_stt])
                                for h in range(2):
                                    act(pb[:, h, :], sm[:, h, :], AF.Exp, [b_sm, b_stt], [b_pb, b_stt],
                                        bias=stt_[:, 2 + h:3 + h], scale=0.125, accum=stt_[:, 4 + h:5 + h])
                                for h in range(2):
                                    for kh in range(2):
                                        i4 = h * 2 + kh
                                        tr(pst[:, i4 * 128:(i4 + 1) * 128], pb[:, h, kh * 128:(kh + 1) * 128],
                                           ident_b[:], [b_pb, b_identb], [b_pst])
                                cp("act", ptb[:], pst[:, 0:512].rearrange("p (a b) -> p a b", a=4), [b_pst], [b_ptb])
                                for h in range(2):
                                    for kh in range(2):
                                        if c == 0:
                                            vt, b_vt = Vn[:, R + kh, h * 64:(h + 1) * 64], b_Vn
                                        elif c == 1:
                                            vt, b_vt = V4[:, (R + kh) * 4 + r, h * 64:(h + 1) * 64], b_V4
                                        else:
                                            vt, b_vt = V16[:, (R + kh) * 16 + r, h * 64:(h + 1) * 64], b_V16
                                        mm(ps1[:, h * 64:(h + 1) * 64], ptb[:, h * 2 + kh, :], vt,
                                           kh == 0, kh == 1, [b_ptb, b_vt], [b_ps1])
                                stg, b_stg = stages[nblk % 4]
                                nblk += 1
                                rcp(stt_[:, 6:8], stt_[:, 4:6], [b_stt], [b_stt])
                                tt("dve", stg[:, 0:128].rearrange("p (a b) -> p a b", a=2),
                                   ps1[:, 0:128].rearrange("p (a b) -> p a b", a=2),
                                   stt_[:, 6:8].unsqueeze(2).broadcast_to([128, 2, 64]), ALU.mult,
                                   [b_ps1, b_stt], [b_stg])
                                act(stt_[:, 8:10], stt_[:, 4:6], AF.Ln, [b_stt], [b_stt])
                                tt("dve", stg[:, 128:130], stt_[:, 8:10], stt_[:, 2:4], ALU.subtract,
                                   [b_stt], [b_stg])
                                ev = dma("sp", onat_d[c, ssl(oq, 128, dil), 0:130], stg[:, 0:130], [b_stg], [],
                                         track=b_stg, extra=rd_ev)
                                wr_ev.append(ev)
                    sub("a2", "d")
                    rd_ev = []
                    for c in range(3):
                        ot, b_ot = onat[c]
                        ev = dma("sp", ot[:, :, 0:130], onat_d[c].rearrange("(t q) f -> q t f", q=128)[:, :, 0:130],
                                 [], [b_ot], extra=wr_ev)
                        rd_ev.append(ev)
                    l0, l1, l2 = (onat[c][0][:, :, 128:130] for c in range(3))
                    bo = [onat[c][1] for c in range(3)]
                    tt("dve", mg[:, :, 0:2], l0, l1, ALU.max, [bo[0], bo[1]], [b_mg])
                    tt("dve", mg[:, :, 0:2], mg[:, :, 0:2], l2, ALU.max, [bo[2], b_mg], [b_mg])
                    for c in range(3):
                        tt("dve", wgt[:, c], onat[c][0][:, :, 128:130], mg[:, :, 0:2], ALU.subtract,
                           [bo[c], b_mg], [b_wgt])
                    act(wgt[:], wgt[:], AF.Exp, [b_wgt], [b_wgt])
                    tt("dve", mg[:, :, 2:4], wgt[:, 0], wgt[:, 1], ALU.add, [b_wgt], [b_mg])
                    tt("dve", mg[:, :, 2:4], mg[:, :, 2:4], wgt[:, 2], ALU.add, [b_wgt, b_mg], [b_mg])
                    rcp(mg[:, :, 4:6], mg[:, :, 2:4], [b_mg], [b_mg])
                    for c in range(3):
                        tt("dve", wgt[:, c], wgt[:, c], mg[:, :, 4:6], ALU.mult, [b_wgt, b_mg], [b_wgt])
                    o0, o1, o2 = (onat[c][0][:, :, 0:128] for c in range(3))
                    for c in range(3):
                        ov = onat[c][0][:, :, 0:128].rearrange("p t (h d) -> p t h d", h=2)
                        tt("dve", ov, ov, wgt[:, c].unsqueeze(3).broadcast_to([128, 16, 2, 64]), ALU.mult,
                           [bo[c], b_wgt], [bo[c]])
                    tt("dve", o0, o0, o1, ALU.add, [bo[0], bo[1]], [bo[0]])
                    tt("dve", o0, o0, o2, ALU.add, [bo[0], bo[2]], [bo[0]])
                    tt("pool", o1, o0, o0, ALU.mult, [bo[0]], [bo[1]])
                    red(sq16[:], o1, ALU.add, [bo[1]], [b_sq16])
                    tt("dve", ssq_at[:], ssq_at[:], sq16[:], ALU.add, [b_sq16, b_ssqat], [b_ssqat])
                    cp("act", attb[:], o0, [bo[0]], [b_attb])
                    for t in range(16):
                        tr(pst[:, (t % 8) * 128:(t % 8 + 1) * 128], attb[:, t, :], ident_b[:],
                           [b_attb, b_identb], [b_pst])
                        if t % 8 == 7:
                            cp("act", attT[:, (t - 7) * 128:(t + 1) * 128], pst[:, :], [b_pst], [b_attT])
                    dma("sp", mixT_d[p], attT[:], [b_attT], [], track=b_attT)
            fw.barrier()

            phase("a1")
            with ExitStack() as ph:
                wxbc, b_wxbc = T(ph, "wxbc", [128, 8, 2048], BF16)
                wz, b_wz = T(ph, "wz", [128, 8, 1024], BF16)
                wdt, b_wdt = T(ph, "wdt", [128, 8, 16], BF16)
                cw, b_cw = T(ph, "cw", [128, 16, 4], F32)
                cb, b_cb = T(ph, "cb", [128, 16], F32)
                dtb, b_dtb = T(ph, "dtb", [128, 16], F32)
                Ab, b_Ab = T(ph, "Ab", [128, 16], F32)
                Db, b_Db = T(ph, "Db", [128, 16], F32)
                tri, b_tri = T(ph, "tri", [128, 128], F32)
                mneg, b_mneg = T(ph, "mneg", [128, 128], BF16)
                pre, b_pre = T(ph, "pre", [128, 515], F32)
                carry, b_carry = T(ph, "carry", [128, 16, 3], F32)
                zst, b_zst = T(ph, "zst", [NS, 512], F32)
                acc, b_acc = T(ph, "acc", [128, 512], F32)
                xc, b_xc = T(ph, "xc", [128, 16, 512], BF16)
                stT, b_stT = T(ph, "stT", [128, 1024], F32)
                stTb, b_stTb = T(ph, "stTb", [128, 1024], BF16)
                sm_, b_sm = T(ph, "smalls", [128, 12, 16], F32)
                cumT, b_cumT = T(ph, "cumT", [16, 2, 128], F32)
                x_tm, b_xtm = T(ph, "x_tm", [128, 1024], BF16)
                B_tm, b_Btm = T(ph, "B_tm", [128, 512], BF16)
                xw, b_xw = T(ph, "xw", [128, 1024], BF16)
                xdt, b_xdt = T(ph, "xdt", [128, 1024], BF16)
                ee, b_ee = T(ph, "ee", [128, 512], F32)
                MT, b_MT = T(ph, "MT", [128, 4, 128], BF16)
                y1, b_y1 = T(ph, "y1", [128, 1024], F32)
                y2, b_y2 = T(ph, "y2", [128, 1024], F32)
                sz, b_sz = T(ph, "sz", [128, 1024], F32)
                yb, b_yb = T(ph, "yb", [128, 1024], BF16)
                yT, b_yT = T(ph, "yT", [128, 8, 128], BF16)
                gs, b_gs = T(ph, "gs", [128, 16], F32)
                for kc in range(8):
                    dma("pool", wxbc[:, kc, :], w_in_k[:, kc, 4096:6144], [b_in], [b_wxbc])
                    dma("pool", wz[:, kc, :], w_in_k[:, kc, 3072:4096], [b_in], [b_wz])
                dma("pool", wdt[:], w_in_k[:, :, 6144:6160], [b_in], [b_wdt])
                dma("pool", mneg[:], k_mneg, [b_in], [b_mneg])
                dma("sp", cw[:], conv_wT, [b_in], [b_cw])
                dma("sp", cb[:], conv_bT, [b_in], [b_cb])
                dma("sp", tri[:], k_tri, [b_in], [b_tri])
                dma("sp", dtb[:], dt_bias.partition_broadcast(128), [b_in], [b_dtb])
                dma("sp", Ab[:], a_log.partition_broadcast(128), [b_in], [b_Ab])
                dma("sp", Db[:], d_skip.partition_broadcast(128), [b_in], [b_Db])
                act(Ab[:], Ab[:], AF.Exp, [b_Ab], [b_Ab])
                ts("dve", Ab[:], Ab[:], -1.0, None, ALU.mult, None, [b_Ab], [b_Ab])
                mset("dve", stT[:], 0.0, [b_stT])
                mset("pool", stTb[:], 0.0, [b_stTb])
                mset("dve", carry[:], 0.0, [b_carry])

                def softplus_dt(npar, src_ap, rd):
                    tt("dve", sm_[:npar, 2], src_ap, dtb[:npar, :], ALU.add, rd + [b_dtb], [b_sm])
                    ts("dve", sm_[:npar, 3], sm_[:npar, 2], -1.0, None, ALU.mult, None, [b_sm], [b_sm])
                    tt("dve", sm_[:npar, 3], sm_[:npar, 3], sm_[:npar, 2], ALU.min, [b_sm], [b_sm])
                    act(sm_[:npar, 3], sm_[:npar, 3], AF.Exp, [b_sm], [b_sm])
                    act(sm_[:npar, 3], sm_[:npar, 3], AF.Ln, [b_sm], [b_sm], bias=1.0)
                    ts("dve", sm_[:npar, 2], sm_[:npar, 2], 0.0, None, ALU.max, None, [b_sm], [b_sm])
                    tt("dve", sm_[:npar, 0], sm_[:npar, 2], sm_[:npar, 3], ALU.add, [b_sm], [b_sm])
                    tt("dve", sm_[:npar, 1], sm_[:npar, 0], Ab[:npar, :], ALU.mult, [b_sm, b_Ab], [b_sm])

                for grp in range(8):
                    own = grp >= 4
                    tok0 = grp * 512
                    ncc = 16 if grp >= 3 else 12
                    for cc in range(ncc):
                        for kc in range(8):
                            mm(ps0[:, :], wxbc[:, kc, cc * 128:(cc + 1) * 128], hT[:, kc, tok0:tok0 + 512],
                               kc == 0, kc == 7, [b_wxbc, b_hT], [b_ps0])
                        cp("dve", pre[:, 0:3], carry[:, cc, :], [b_carry], [b_pre])
                        cp("act", pre[:, 3:515], ps0[:, :], [b_ps0], [b_pre])
                        ts("dve", acc[:], pre[:, 3:515], cw[:, cc, 3:4], cb[:, cc:cc + 1], ALU.mult, ALU.add,
                           [b_pre, b_cw, b_cb], [b_acc])
                        for k in (2, 1, 0):
                            stt(acc[:], pre[:, k:k + 512], cw[:, cc, k:k + 1], acc[:], ALU.mult, ALU.add,
                                [b_pre, b_cw, b_acc], [b_acc])
                        act(xc[:, cc, :], acc[:], AF.Silu, [b_acc], [b_xc])
                        if grp == 3:
                            ts("dve", carry[:, cc, :], pre[:, 512:515], halo[:, 0:1], None, ALU.mult, None,
                               [b_pre, b_halo], [b_carry])
                        else:
                            cp("dve", carry[:, cc, :], pre[:, 512:515], [b_pre], [b_carry])
                    if grp == 7:
                        for k in range(3):
                            dma("sp", convp[k, :].rearrange("(cc q) -> q cc", q=128), carry[:, :, k], [b_carry], [],
                                track=b_carry, slow=True)
                    for j in range(4):
                        ck = grp * 4 + j
                        csl = slice(j * 128, (j + 1) * 128)
                        tsl = slice(ck * 128, (ck + 1) * 128)
                        for kc in range(8):
                            mm(ps1[:, 0:16], hT[:, kc, tsl], wdt[:, kc, :], kc == 0, kc == 7, [b_hT, b_wdt], [b_ps1])
                        softplus_dt(128, ps1[:, 0:16], [b_ps1])
                        mm(ps1[:, 16:32], tri[:], sm_[:, 1], True, True, [b_tri, b_sm], [b_ps1])
                        mm(ps1[:, 32:48], ones_f[:], sm_[:, 1], True, True, [b_ones, b_sm], [b_ps1])
                        cp("dve", sm_[:, 4], ps1[:, 16:32], [b_ps1], [b_sm])
                        cp("dve", sm_[:, 5], ps1[:, 32:48], [b_ps1], [b_sm])
                        tt("dve", sm_[:, 6], sm_[:, 5], sm_[:, 4], ALU.subtract, [b_sm], [b_sm])
                        act(sm_[:, 6], sm_[:, 6], AF.Exp, [b_sm], [b_sm])
                        tt("dve", sm_[:, 6], sm_[:, 6], sm_[:, 0], ALU.mult, [b_sm], [b_sm])
                        act(sm_[:, 7], sm_[:, 4], AF.Exp, [b_sm], [b_sm])
                        act(sm_[:, 8], sm_[:, 5], AF.Exp, [b_sm], [b_sm])
                        for cc in range(8):
                            tr(pst[:, cc * 128:(cc + 1) * 128], xc[:, cc, csl], ident_b[:], [b_xc, b_identb], [b_pst])
                        cp("act", x_tm[:], pst[:, :], [b_pst], [b_xtm])
                        for cc in range(4):
                            tr(pst[:, cc * 128:(cc + 1) * 128], xc[:, 8 + cc, csl], ident_b[:], [b_xc, b_identb], [b_pst])
                        cp("act", B_tm[:], pst[:, 0:512], [b_pst], [b_Btm])
                        tt("dve", xw[:].rearrange("p (h d) -> p h d", h=16), x_tm[:].rearrange("p (h d) -> p h d", h=16),
                           sm_[:, 6].unsqueeze(2).broadcast_to([128, 16, 64]), ALU.mult, [b_xtm, b_sm], [b_xw])
                        if own:
                            tt("pool", xdt[:].rearrange("p (h d) -> p h d", h=16),
                               x_tm[:].rearrange("p (h d) -> p h d", h=16),
                               sm_[:, 0].unsqueeze(2).broadcast_to([128, 16, 64]), ALU.mult, [b_xtm, b_sm], [b_xdt])
                            mm(ps1[0:16, 64:192], sm_[:, 1], tri[:], True, True, [b_sm, b_tri], [b_ps1])
                            cp("dve", cumT[:, 0, :], ps1[0:16, 64:192], [b_ps1], [b_cumT])
                            ts("dve", cumT[:, 1, :], ps1[0:16, 64:192], -1.0, None, ALU.mult, None, [b_ps1], [b_cumT])
                            for g in range(4):
                                mm(ps45[:, g * 256:(g + 1) * 256], xc[:, 12 + g, csl], stTb[:, g * 256:(g + 1) * 256],
                                   True, True, [b_xc, b_stTb], [b_ps45])
                            tt("dve", y1[:].rearrange("p (h d) -> p h d", h=16),
                               ps45[:, :].rearrange("p (h d) -> p h d", h=16),
                               sm_[:, 7].unsqueeze(2).broadcast_to([128, 16, 64]), ALU.mult, [b_ps45, b_sm], [b_y1])
                        for g in range(4):
                            mm(ps23[:, g * 256:(g + 1) * 256], B_tm[:, g * 128:(g + 1) * 128],
                               xw[:, g * 256:(g + 1) * 256], True, True, [b_Btm, b_xw], [b_ps23])
                        tt("dve", stT[:].rearrange("p (h d) -> p h d", h=16), stT[:].rearrange("p (h d) -> p h d", h=16),
                           sm_[:, 8].unsqueeze(2).broadcast_to([128, 16, 64]), ALU.mult, [b_stT, b_sm], [b_stT])
                        tt("dve", stT[:], stT[:], ps23[:, :], ALU.add, [b_stT, b_ps23], [b_stT])
                        if ck == 15:
                            ts("dve", stT[:], stT[:], halo[:, 0:1], None, ALU.mult, None, [b_stT, b_halo], [b_stT])
                        cp("act", stTb[:], stT[:], [b_stT], [b_stTb])
                        if not own:
                            continue
                        for g in range(4):
                            mm(ps6[:, g * 128:(g + 1) * 128], xc[:, 8 + g, csl], xc[:, 12 + g, csl], True, True,
                               [b_xc], [b_ps6])
                        for g in range(4):
                            for hh in range(4):
                                h = g * 4 + hh
                                o_ = ps0[:, hh * 128:(hh + 1) * 128]
                                mm(o_, sel[:, h, :], cumT[:, 0, :], True, False, [b_sel, b_cumT], [b_ps0])
                                mm(o_, cumT[:, 1, :], sel[:, h, :], False, False, [b_sel, b_cumT], [b_ps0])
                                mm(o_, ident_b[:], mneg[:], False, True, [b_identb, b_mneg], [b_ps0])
                            act(ee[:], ps0[:, :], AF.Exp, [b_ps0], [b_ee])
                            tt("dve", MT[:], ee[:].rearrange("p (a b) -> p a b", a=4),
                               ps6[:, g * 128:(g + 1) * 128].unsqueeze(1).broadcast_to([128, 4, 128]), ALU.mult,
                               [b_ee, b_ps6], [b_MT])
                            for hh in range(4):
                                h = g * 4 + hh
                                mm(ps23[:, h * 64:(h + 1) * 64], MT[:, hh, :], xdt[:, h * 64:(h + 1) * 64], True, True,
                                   [b_MT, b_xdt], [b_ps23])
                        tt("dve", y1[:], y1[:], ps23[:, :], ALU.add, [b_y1, b_ps23], [b_y1])
                        tt("dve", y2[:].rearrange("p (h d) -> p h d", h=16), x_tm[:].rearrange("p (h d) -> p h d", h=16),
                           Db[:].unsqueeze(2).broadcast_to([128, 16, 64]), ALU.mult, [b_xtm, b_Db], [b_y2])
                        tt("dve", y1[:], y1[:], y2[:], ALU.add, [b_y1, b_y2], [b_y1])
                        for hf in range(2):
                            for kc in range(8):
                                mm(ps45[:, hf * 512:(hf + 1) * 512], hT[:, kc, tsl], wz[:, kc, hf * 512:(hf + 1) * 512],
                                   kc == 0, kc == 7, [b_hT, b_wz], [b_ps45])
                        act(sz[:], ps45[:, :], AF.Silu, [b_ps45], [b_sz])
                        tt("dve", y1[:], y1[:], sz[:], ALU.mult, [b_y1, b_sz], [b_y1])
                        tt("pool", y2[:], y1[:], y1[:], ALU.mult, [b_y1], [b_y2])
                        red(gs[:, 0:4], y2[:].rearrange("p (g d) -> p g d", g=4), ALU.add, [b_y2], [b_gs])
                        ts("dve", gs[:, 4:8], gs[:, 0:4], 1.0 / 256, EPS, ALU.mult, ALU.add, [b_gs], [b_gs])
                        act(gs[:, 8:12], gs[:, 4:8], AF.Sqrt, [b_gs], [b_gs])
                        rcp(gs[:, 12:16], gs[:, 8:12], [b_gs], [b_gs])
                        tt("dve", yb[:].rearrange("p (g d) -> p g d", g=4), y1[:].rearrange("p (g d) -> p g d", g=4),
                           gs[:, 12:16].unsqueeze(2).broadcast_to([128, 4, 256]), ALU.mult, [b_y1, b_gs], [b_yb])
                        for kc in range(8):
                            tr(pst[:, kc * 128:(kc + 1) * 128], yb[:, kc * 128:(kc + 1) * 128], ident_b[:],
                               [b_yb, b_identb], [b_pst])
                        cp("act", yT[:], pst[:, :].rearrange("p (a b) -> p a b", a=8), [b_pst], [b_yT])
                        oc = (ck - 16) * 128
                        dma("sp", mixT_d[8:16, :, oc:oc + 128].rearrange("k q t -> q k t"), yT[:], [b_yT], [],
                            track=b_yT)
                for kc in range(8):
                    tr(ps45[:, (kc % 4) * 128:(kc % 4 + 1) * 128], stT[:, kc * 128:(kc + 1) * 128], ident_f[:],
                       [b_stT, b_identf], [b_ps45])
                    if kc % 4 == 3:
                        cp("dve", y1[:, 0:512], ps45[:, 0:512], [b_ps45], [b_y1])
                        dma("sp", ssmp[(kc - 3) * 128:(kc + 1) * 128, :].rearrange("(a q) n -> q a n", q=128),
                            y1[:, 0:512].rearrange("p (a n) -> p a n", a=4), [b_y1], [], track=b_y1)
                for (c0, wt, bw, width, o0) in ((0, wz, b_wz, 1024, 0), (0, wxbc, b_wxbc, 2048, 1024)):
                    for hf in range(width // 512):
                        for kc in range(8):
                            mm(ps0[0:NS, :], hsT[:, kc, :], wt[:, kc, hf * 512:(hf + 1) * 512], kc == 0, kc == 7,
                               [b_hsT, bw], [b_ps0])
                        cp("act", zst[:], ps0[0:NS, :], [b_ps0], [b_zst])
                        dma("sp", zxds_d[:, o0 + hf * 512:o0 + (hf + 1) * 512], zst[:], [b_zst], [], track=b_zst)
                for kc in range(8):
                    mm(ps0[0:NS, 0:16], hsT[:, kc, :], wdt[:, kc, :], kc == 0, kc == 7, [b_hsT, b_wdt], [b_ps0])
                cp("act", zst[:, 0:16], ps0[0:NS, 0:16], [b_ps0], [b_zst])
                dma("sp", zxds_d[:, 3072:3088], zst[:, 0:16], [b_zst], [], track=b_zst)
        fw.barrier()

        phase("s")
        with ExitStack() as ph:
            Kc, b_Kc = T(ph, "Kc", [128, 3, D], F32)
            Vc, b_Vc = T(ph, "Vc", [128, 3, D], F32)
            prod, b_prod = T(ph, "prod", [128, 3, D], F32)
            sc, b_sc = T(ph, "sc", [128, 48], F32)
            sall, b_sall = T(ph, "sall", [16, 3, 129], F32)
            pn, b_pn = T(ph, "pn", [16, 3, 129], F32)
            sst, b_sst = T(ph, "sst", [16, 8], F32)
            pT, b_pT = T(ph, "pT", [128, 48], F32)
            sself, b_sself = T(ph, "sself", [NS, 16], F32)
            sselfT, b_sselfT = T(ph, "sselfT", [16, NS], F32)
            pselfT, b_pselfT = T(ph, "pselfT", [16, NS], F32)
            pself, b_pself = T(ph, "pself", [NS, 16], F32)
            onehot, b_onehot = T(ph, "onehot", [128, NS, NS], F32)
            tmp_s, b_tmps = T(ph, "tmp_s", [NS, D], F32)
            attn_s, b_attns = T(ph, "attn_s", [NS, D], F32)
            st4, b_st4 = T(ph, "st4s", [NS, 4], F32)
            junk, b_junk = T(ph, "junks", [NS, D], BF16)
            qkv_s, b_qkvs = T(ph, "qkv_s", [NS, 3, D], F32)
            zxd_s, b_zxds = T(ph, "zxd_s", [NS, 3088], F32)
            mixs, b_mixs = T(ph, "mixs", [NS, 2 * D], F32)
            dma("sp", qkv_s[:], qkvs_d, [], [b_qkvs])
            dma("sp", zxd_s[:], zxds_d, [], [b_zxds])
            mset("dve", onehot[:], 0.0, [b_onehot])
            for b in range(NS):
                mset("dve", onehot[:, b, b:b + 1], 1.0, [b_onehot])
            tt("dve", tmp_s[:], qkv_s[:, 0, :], qkv_s[:, 1, :], ALU.mult, [b_qkvs], [b_tmps])
            red(sself[:], tmp_s[:].rearrange("p (h d) -> p h d", h=16), ALU.add, [b_tmps], [b_sself])
            tr(ps6[0:16, 0:16], sself[:], ident_f[0:16, 0:16], [b_sself, b_identf], [b_ps6])
            cp("dve", sselfT[:], ps6[0:16, 0:16], [b_ps6], [b_sselfT])
            dma("sp", ks[:, 2047, :], qkv_s[:, 1, :], [b_qkvs], [], track=b_qkvs)
            dma("sp", vs[:, 2047, :], qkv_s[:, 2, :], [b_qkvs], [], track=b_qkvs)
            for b in range(NS):
                for (dst, bd, src) in ((Kc, b_Kc, cache_k), (Vc, b_Vc, cache_v)):
                    dma("sp", dst[:, 0, :], src[b, 1920:2048, :], [b_in], [bd])
                    dma("sp", dst[:, 1, :], src[b, 1536:2048:4, :], [b_in], [bd])
                    dma("sp", dst[:, 2, :], src[b, 0:2048:16, :], [b_in], [bd])
                for hf in range(2):
                    mm(ps23[:, hf * 512:(hf + 1) * 512], sel[:, b, :], qkv_s[:, 0, hf * 512:(hf + 1) * 512], True, True,
                       [b_sel, b_qkvs], [b_ps23])
                tt("dve", prod[:], Kc[:], ps23[:, :].unsqueeze(1).broadcast_to([128, 3, D]), ALU.mult,
                   [b_Kc, b_ps23], [b_prod])
                red(sc[:], prod[:].rearrange("p c (h d) -> p (c h) d", h=16), ALU.add, [b_prod], [b_sc])
                for c in range(3):
                    tr(ps6[0:16, c * 128:(c + 1) * 128], sc[:, c * 16:(c + 1) * 16], ident_f[:], [b_sc, b_identf], [b_ps6])
                cp("dve", sall[:, :, 0:128], ps6[0:16, 0:384].rearrange("p (c k) -> p c k", c=3), [b_ps6], [b_sall])
                cp("dve", sall[:, :, 128:129], sselfT[:, b:b + 1].unsqueeze(1).broadcast_to([16, 3, 1]),
                   [b_sselfT], [b_sall])
                red(sst[:, 0:1], sall[:].rearrange("p c k -> p (c k)"), ALU.max, [b_sall], [b_sst], negate=True)
                ts("dve", sst[:, 1:2], sst[:, 0:1], 0.125, None, ALU.mult, None, [b_sst], [b_sst])
                act(pn[:].rearrange("p c k -> p (c k)"), sall[:].rearrange("p c k -> p (c k)"), AF.Exp,
                    [b_sall, b_sst], [b_pn, b_sst], bias=sst[:, 1:2], scale=0.125, accum=sst[:, 2:3])
                rcp(sst[:, 3:4], sst[:, 2:3], [b_sst], [b_sst])
                ts("dve", pn[:].rearrange("p c k -> p (c k)"), pn[:].rearrange("p c k -> p (c k)"), sst[:, 3:4], None,
                   ALU.mult, None, [b_pn, b_sst], [b_pn])
                red(pselfT[:, b:b + 1], pn[:, :, 128:129].rearrange("p c k -> p k c"), ALU.add, [b_pn], [b_pselfT])
                for c in range(3):
                    tr(ps6[:, 384 + c * 16:384 + (c + 1) * 16], pn[:, c, 0:128], ident_f[0:16, 0:16],
                       [b_pn, b_identf], [b_ps6])
                cp("dve", pT[:], ps6[:, 384:432], [b_ps6], [b_pT])
                tt("dve", prod[:].rearrange("p c (h d) -> p (c h) d", h=16),
                   Vc[:].rearrange("p c (h d) -> p (c h) d", h=16),
                   pT[:].unsqueeze(2).broadcast_to([128, 48, 64]), ALU.mult, [b_Vc, b_pT], [b_prod])
                for c in range(3):
                    for hf in range(2):
                        mm(ps45[0:NS, hf * 512:(hf + 1) * 512], onehot[:, b, :], prod[:, c, hf * 512:(hf + 1) * 512],
                           b == 0 and c == 0, b == NS - 1 and c == 2, [b_onehot, b_prod], [b_ps45])
            tr(ps6[0:16, 0:16], pselfT[:], ident_f[0:16, 0:16], [b_pselfT, b_identf], [b_ps6])
            cp("dve", pself[:], ps6[0:16, 0:16], [b_ps6], [b_pself])
            tt("dve", tmp_s[:].rearrange("p (h d) -> p h d", h=16), qkv_s[:, 2, :].rearrange("p (h d) -> p h d", h=16),
               pself[:].unsqueeze(2).broadcast_to([NS, 16, 64]), ALU.mult, [b_qkvs, b_pself], [b_tmps])
            tt("dve", attn_s[:], tmp_s[:], ps45[0:NS, :], ALU.add, [b_tmps, b_ps45], [b_attns])
            rmsnorm_stats(NS, attn_s[:], b_attns, st4, junk, b_junk, b_st4, D)
            ts("dve", mixs[:, 0:D], attn_s[:], st4[:, 3:4], None, ALU.mult, None, [b_attns, b_st4], [b_mixs])

            xe, b_xe = T(ph, "xe", [NS, 2048], F32)
            cwb, b_cwb = T(ph, "cwb", [NS, 2048], F32)
            cbb, b_cbb = T(ph, "cbb", [NS, 2048], F32)
            xbc, b_xbc = T(ph, "xbc", [NS, 2048], F32)
            t2, b_t2 = T(ph, "t2", [NS, 2048], F32)
            dtb, b_dtb = T(ph, "dtbs", [NS, 16], F32)
            Ab, b_Ab = T(ph, "Abs", [NS, 16], F32)
            Db, b_Db = T(ph, "Dbs", [NS, 16], F32)
            sm_, b_sm = T(ph, "smalls_s", [NS, 6, 16], F32)
            dtx, b_dtx = T(ph, "dtx", [NS, 2, D], F32)
            colT, b_colT = T(ph, "colT", [128, 2, 8, NS], F32)
            hs, b_hs = T(ph, "hs", [128, 8, 128], F32)
            hn, b_hn = T(ph, "hn", [128, 8, 128], F32)
            bcb, b_bcb = T(ph, "bcb", [128, 2, 512], F32)
            yTs, b_yTs = T(ph, "yTs", [128, 8, NS], F32)
            ys, b_ys = T(ph, "ys", [NS, D], F32)
            sz, b_sz = T(ph, "szs", [NS, D], F32)
            gs, b_gs = T(ph, "gss", [NS, 16], F32)
            dma("sp", convs[:, 2, :], zxd_s[:, 1024:3072], [b_zxds], [], track=b_zxds)
            dma("sp", cbb[:], conv_b.partition_broadcast(NS), [b_in], [b_cbb])
            dma("sp", dtb[:], dt_bias.partition_broadcast(NS), [b_in], [b_dtb])
            dma("sp", Ab[:], a_log.partition_broadcast(NS), [b_in], [b_Ab])
            dma("sp", Db[:], d_skip.partition_broadcast(NS), [b_in], [b_Db])
            act(Ab[:], Ab[:], AF.Exp, [b_Ab], [b_Ab])
            ts("dve", Ab[:], Ab[:], -1.0, None, ALU.mult, None, [b_Ab], [b_Ab])
            for k in range(4):
                dma("sp", cwb[:], conv_w[k:k + 1, :].partition_broadcast(NS), [b_in], [b_cwb])
                if k < 3:
                    dma("sp", xe[:], st_conv[:, k, :], [b_in], [b_xe])
                    xk, bxk = xe[:], b_xe
                else:
                    xk, bxk = zxd_s[:, 1024:3072], b_zxds
                if k == 0:
                    tt("dve", xbc[:], xk, cwb[:], ALU.mult, [bxk, b_cwb], [b_xbc])
                else:
                    tt("dve", t2[:], xk, cwb[:], ALU.mult, [bxk, b_cwb], [b_t2])
                    tt("dve", xbc[:], xbc[:], t2[:], ALU.add, [b_xbc, b_t2], [b_xbc])
            tt("dve", xbc[:], xbc[:], cbb[:], ALU.add, [b_xbc, b_cbb], [b_xbc])
            act(xbc[:], xbc[:], AF.Silu, [b_xbc], [b_xbc])
            tt("dve", sm_[:, 2], zxd_s[:, 3072:3088], dtb[:], ALU.add, [b_zxds, b_dtb], [b_sm])
            ts("dve", sm_[:, 3], sm_[:, 2], -1.0, None, ALU.mult, None, [b_sm], [b_sm])
            tt("dve", sm_[:, 3], sm_[:, 3], sm_[:, 2], ALU.min, [b_sm], [b_sm])
            act(sm_[:, 3], sm_[:, 3], AF.Exp, [b_sm], [b_sm])
            act(sm_[:, 3], sm_[:, 3], AF.Ln, [b_sm], [b_sm], bias=1.0)
            ts("dve", sm_[:, 2], sm_[:, 2], 0.0, None, ALU.max, None, [b_sm], [b_sm])
            tt("dve", sm_[:, 0], sm_[:, 2], sm_[:, 3], ALU.add, [b_sm], [b_sm])
            tt("dve", sm_[:, 1], sm_[:, 0], Ab[:], ALU.mult, [b_sm, b_Ab], [b_sm])
            act(sm_[:, 1], sm_[:, 1], AF.Exp, [b_sm], [b_sm])
            tt("dve", dtx[:, 0, :].rearrange("p (h d) -> p h d", h=16), xbc[:, 0:D].rearrange("p (h d) -> p h d", h=16),
               sm_[:, 0].unsqueeze(2).broadcast_to([NS, 16, 64]), ALU.mult, [b_xbc, b_sm], [b_dtx])
            cp("dve", dtx[:, 1, :].rearrange("p (h d) -> p h d", h=16),
               sm_[:, 1].unsqueeze(2).broadcast_to([NS, 16, 64]), [b_sm], [b_dtx])
            for a in range(2):
                for kc in range(8):
                    tr(ps6[:, (a * 8 + kc) * 16:(a * 8 + kc + 1) * 16], dtx[:, a, kc * 128:(kc + 1) * 128],
                       ident_f[0:16, 0:16], [b_dtx, b_identf], [b_ps6])
            cp("dve", colT[:].rearrange("p a k s -> p (a k s)"), ps6[:, 0:256], [b_ps6], [b_colT])
            for b in range(NS):
                dma("sp", hs[:], st_ssm[b].rearrange("(j q) n -> q j n", q=128), [b_in], [b_hs])
                for a in range(2):
                    mm(ps23[:, a * 512:(a + 1) * 512], sel[:, b, :], xbc[:, D + a * 512:D + (a + 1) * 512], True, True,
                       [b_sel, b_xbc], [b_ps23])
                cp("act", bcb[:].rearrange("p a n -> p (a n)"), ps23[:, :], [b_ps23], [b_bcb])
                tt("dve", hn[:], hs[:], colT[:, 1, :, b:b + 1].broadcast_to([128, 8, 128]), ALU.mult,
                   [b_hs, b_colT], [b_hn])
                Bv = bcb[:, 0, :].rearrange("p (g n) -> p g n", g=4)
                Cv = bcb[:, 1, :].rearrange("p (g n) -> p g n", g=4)
                for j in range(8):
                    stt(hn[:, j, :], Bv[:, j // 2, :], colT[:, 0, j, b:b + 1], hn[:, j, :], ALU.mult, ALU.add,
                        [b_bcb, b_colT, b_hn], [b_hn])
                dma("sp", ssms[b].rearrange("(j q) n -> q j n", q=128), hn[:], [b_hn], [], track=b_hn)
                for j in range(8):
                    tt("dve", hs[:, j, :], hn[:, j, :], Cv[:, j // 2, :], ALU.mult, [b_hn, b_bcb], [b_hs])
                red(yTs[:, :, b], hs[:], ALU.add, [b_hs], [b_yTs])
            for kc in range(8):
                tr(ps6[0:NS, kc * 128:(kc + 1) * 128] if kc < 4 else ps1[0:NS, (kc - 4) * 128:(kc - 3) * 128],
                   yTs[:, kc, :], ident_f[:], [b_yTs, b_identf], [b_ps6, b_ps1])
            cp("dve", ys[:, 0:512], ps6[0:NS, :], [b_ps6], [b_ys])
            cp("dve", ys[:, 512:1024], ps1[0:NS, :], [b_ps1], [b_ys])
            tt("dve", t2[:, 0:D].rearrange("p (h d) -> p h d", h=16), xbc[:, 0:D].rearrange("p (h d) -> p h d", h=16),
               Db[:].unsqueeze(2).broadcast_to([NS, 16, 64]), ALU.mult, [b_xbc, b_Db], [b_t2])
            tt("dve", ys[:], ys[:], t2[:, 0:D], ALU.add, [b_ys, b_t2], [b_ys])
            act(sz[:], zxd_s[:, 0:D], AF.Silu, [b_zxds], [b_sz])
            tt("dve", ys[:], ys[:], sz[:], ALU.mult, [b_ys, b_sz], [b_ys])
            tt("dve", t2[:, 0:D], ys[:], ys[:], ALU.mult, [b_ys], [b_t2])
            red(gs[:, 0:4], t2[:, 0:D].rearrange("p (g d) -> p g d", g=4), ALU.add, [b_t2], [b_gs])
            ts("dve", gs[:, 4:8], gs[:, 0:4], 1.0 / 256, EPS, ALU.mult, ALU.add, [b_gs], [b_gs])
            act(gs[:, 8:12], gs[:, 4:8], AF.Sqrt, [b_gs], [b_gs])
            rcp(gs[:, 12:16], gs[:, 8:12], [b_gs], [b_gs])
            tt("dve", mixs[:, D:2 * D].rearrange("p (g d) -> p g d", g=4), ys[:].rearrange("p (g d) -> p g d", g=4),
               gs[:, 12:16].unsqueeze(2).broadcast_to([NS, 4, 256]), ALU.mult, [b_ys, b_gs], [b_mixs])
            dma("sp", mixs_d, mixs[:], [b_mixs], [], track=b_mixs)
        fw.barrier()

        phase("b1")
        with ExitStack() as ph:
            wout, b_wout = T(ph, "wout", [128, 16, D], BF16)
            wst, b_wst = T(ph, "wst", [128, D], F32)
            gcol, b_gcol = T(ph, "gcol", [128, 16], F32)
            g1b, b_g1b = T(ph, "g1b", [128, D], F32)
            sc2, b_sc2 = T(ph, "sc2", [128, D], F32)
            sh2, b_sh2 = T(ph, "sh2", [128, D], F32)
            g1s, b_g1s = T(ph, "g1s", [NS, D], F32)
            sc2s, b_sc2s = T(ph, "sc2s", [NS, D], F32)
            sh2s, b_sh2s = T(ph, "sh2s", [NS, D], F32)
            mT, b_mT = T(ph, "mT", [128, 16, 128], BF16)
            xt, b_xt = T(ph, "xtb", [128, D], F32)
            sb_, b_sb = T(ph, "sbB", [128, D], F32)
            mix, b_mix = T(ph, "mix", [128, D], F32)
            x1, b_x1 = T(ph, "x1", [128, D], F32)
            junk, b_junk = T(ph, "junkb", [128, D], BF16)
            hb, b_hb = T(ph, "hb2", [128, D], BF16)
            h2T, b_h2T = T(ph, "h2Tt", [128, 8, 128], BF16)
            st4, b_st4 = T(ph, "st4b", [128, 4], F32)
            rsa, b_rsa = T(ph, "rsa", [128, 48], F32)
            mixsb, b_mixsb = T(ph, "mixsb", [NS, 2 * D], BF16)
            mixs, b_mixs = T(ph, "mixs1", [NS, 2 * D], F32)
            dma("sp", mixs[:], mixs_d, [], [b_mixs])
            dma("sp", gcol[:], g_col, [b_in], [b_gcol])
            w_out_k = w_out.rearrange("(kc k) c -> k kc c", k=128)
            for kc in range(16):
                dma("sp", wst[:], w_out_k[:, kc, :], [b_in], [b_wst])
                ts("dve", wout[:, kc, :], wst[:], gcol[:, kc:kc + 1], None, ALU.mult, None, [b_wst, b_gcol], [b_wout])
            dma("sp", g1b[:], modp_d[:, 2 * D:3 * D], [], [b_g1b])
            dma("sp", sh2[:], modp_d[:, 3 * D:4 * D], [], [b_sh2])
            dma("sp", sc2[:], modp_d[:, 4 * D:5 * D], [], [b_sc2])
            dma("sp", g1s[:], mods_d[:, 2 * D:3 * D], [], [b_g1s])
            dma("sp", sh2s[:], mods_d[:, 3 * D:4 * D], [], [b_sh2s])
            dma("sp", sc2s[:], mods_d[:, 4 * D:5 * D], [], [b_sc2s])
            ts("dve", rsa[:, 0:16], ssq_at[:], 1.0 / D, EPS, ALU.mult, ALU.add, [b_ssqat], [b_rsa])
            act(rsa[:, 16:32], rsa[:, 0:16], AF.Sqrt, [b_rsa], [b_rsa])
            rcp(rsa[:, 32:48], rsa[:, 16:32], [b_rsa], [b_rsa])
            for i in range(17):
                smp = i == 16
                npar = NS if smp else 128
                if not smp:
                    dma("sp", mT[:], mixT_d[:, :, i * 128:(i + 1) * 128].rearrange("k q t -> q k t"), [], [b_mT])
                    dma("sp", xt[:], x_tok[S_OWN + i * 128:S_OWN + (i + 1) * 128, :], [b_in], [b_xt])
                    for hf in range(2):
                        for kc in range(8):
                            mm(ps23[:, hf * 512:(hf + 1) * 512], mT[:, kc, :], wout[:, kc, hf * 512:(hf + 1) * 512],
                               kc == 0, kc == 7, [b_mT, b_wout], [b_ps23])
                        for kc in range(8, 16):
                            mm(ps45[:, hf * 512:(hf + 1) * 512], mT[:, kc, :], wout[:, kc, hf * 512:(hf + 1) * 512],
                               kc == 8, kc == 15, [b_mT, b_wout], [b_ps45])
                    cp("act", sb_[:], ps45[:, :], [b_ps45], [b_sb])
                    stt(mix[:], ps23[:, :], rsa[:, 32 + i:33 + i], sb_[:], ALU.mult, ALU.add,
                        [b_ps23, b_rsa, b_sb], [b_mix])
                    gg, bgg, s2, bs2, h2_, bh2 = g1b, b_g1b, sc2, b_sc2, sh2, b_sh2
                else:
                    dma("sp", xt[0:NS, :], x_s, [b_in], [b_xt])
                    cp("act", mixsb[:], mixs[:], [b_mixs], [b_mixsb])
                    for kc in range(16):
                        tr(pst[:, kc * 16:(kc + 1) * 16], mixsb[:, kc * 128:(kc + 1) * 128], ident_b[0:16, 0:16],
                           [b_mixsb, b_identb], [b_pst])
                    cp("act", mT[:, :, 0:NS], pst[:, 0:256].rearrange("p (a b) -> p a b", a=16), [b_pst], [b_mT])
                    for hf in range(2):
                        for kc in range(16):
                            mm(ps23[0:NS, hf * 512:(hf + 1) * 512], mT[:, kc, 0:NS], wout[:, kc, hf * 512:(hf + 1) * 512],
                               kc == 0, kc == 15, [b_mT, b_wout], [b_ps23])
                    cp("act", mix[0:NS, :], ps23[0:NS, :], [b_ps23], [b_mix])
                    gg, bgg, s2, bs2, h2_, bh2 = g1s, b_g1s, sc2s, b_sc2s, sh2s, b_sh2s
                tt("dve", mix[:npar], mix[:npar], gg[:npar], ALU.mult, [b_mix, bgg], [b_mix])
                tt("dve", x1[:npar], mix[:npar], xt[:npar], ALU.add, [b_mix, b_xt], [b_x1])
                if not smp:
                    dma("sp", x1_d[i * 128:(i + 1) * 128, :], x1[:], [b_x1], [], track=b_x1)
                else:
                    dma("sp", x1s_d, x1[0:NS, :], [b_x1], [], track=b_x1)
                rmsnorm_stats(npar, x1[:npar, :], b_x1, st4, junk, b_junk, b_st4, D)
                stt(mix[:npar], x1[:npar], st4[:npar, 3:4], s2[:npar], ALU.mult, ALU.mult, [b_x1, b_st4, bs2], [b_mix])
                tt("dve", hb[:npar], mix[:npar], h2_[:npar], ALU.add, [b_mix, bh2], [b_hb])
                for kc in range(8):
                    tr(pst[:, kc * 128:kc * 128 + npar], hb[:npar, kc * 128:(kc + 1) * 128], ident_b[:npar, :npar],
                       [b_hb, b_identb], [b_pst])
                if not smp:
                    cp("act", h2T[:], pst[:, :].rearrange("p (a b) -> p a b", a=8), [b_pst], [b_h2T])
                    dma("sp", h2T_d[:, :, i * 128:(i + 1) * 128].rearrange("k q t -> q k t"), h2T[:], [b_h2T], [],
                        track=b_h2T)
                else:
                    cp("act", h2sT[:], pst[:, :].rearrange("p (a b) -> p a b", a=8)[:, :, 0:NS], [b_pst], [b_h2sT])
        fw.barrier()

        phase("b2")
        with ExitStack() as ph:
            wup, b_wup = T(ph, "wup", [128, 8, 4 * D], BF16)
            wdn, b_wdn = T(ph, "wdn", [128, 32, D], BF16)
            g2b, b_g2b = T(ph, "g2b", [128, D], F32)
            g2s, b_g2s = T(ph, "g2s", [NS, D], F32)
            gfb, b_gfb = T(ph, "gfb", [128, D], F32)
            h2g, b_h2g = T(ph, "h2g", [128, 8, 256], BF16)
            rl, b_rl = T(ph, "rl", [128, 256], BF16)
            uT, b_uT = T(ph, "uT", [128, 32, 256], BF16)
            x1, b_x1 = T(ph, "x1b", [128, D], F32)
            x2, b_x2 = T(ph, "x2", [128, D], F32)
            yo, b_yo = T(ph, "yo", [128, D], F32)
            junk, b_junk = T(ph, "junkc", [128, D], BF16)
            st4, b_st4 = T(ph, "st4c", [128, 4], F32)
            w_up_k = w_up.rearrange("(kc k) c -> k kc c", k=128)
            w_dn_k = w_down.rearrange("(kc k) c -> k kc c", k=128)
            for kc in range(8):
                for hf in range(2):
                    dma("pool", wup[:, kc, hf * 2048:(hf + 1) * 2048], w_up_k[:, kc, hf * 2048:(hf + 1) * 2048],
                        [b_in], [b_wup])
            for kc in range(32):
                dma("pool", wdn[:, kc, :], w_dn_k[:, kc, :], [b_in], [b_wdn])
            dma("sp", g2b[:], modp_d[:, 5 * D:6 * D], [], [b_g2b])
            dma("sp", g2s[:], mods_d[:, 5 * D:6 * D], [], [b_g2s])
            dma("sp", gfb[:], g_final.partition_broadcast(128), [b_in], [b_gfb])
            for grp in range(9):
                smp = grp == 8
                ntk = NS if smp else 256
                if not smp:
                    dma("sp", h2g[:], h2T_d[:, :, grp * 256:(grp + 1) * 256].rearrange("k q t -> q k t"), [], [b_h2g])
                for fc in range(32):
                    for kc in range(8):
                        rh = h2sT[:, kc, :] if smp else h2g[:, kc, :]
                        mm(ps0[:, 0:ntk], wup[:, kc, fc * 128:(fc + 1) * 128], rh, kc == 0, kc == 7,
                           [b_wup, b_h2g, b_h2sT], [b_ps0])
                    act(rl[:, 0:ntk], ps0[:, 0:ntk], AF.Relu, [b_ps0], [b_rl])
                    tt("pool", uT[:, fc, 0:ntk], rl[:, 0:ntk], rl[:, 0:ntk], ALU.mult, [b_rl], [b_uT])
                for j in range(1 if smp else 2):
                    npar = NS if smp else 128
                    for hf in range(2):
                        for fc in range(32):
                            mm(ps23[:npar, hf * 512:(hf + 1) * 512], uT[:, fc, j * 128:j * 128 + npar],
                               wdn[:, fc, hf * 512:(hf + 1) * 512], fc == 0, fc == 31, [b_uT, b_wdn], [b_ps23])
                    if smp:
                        dma("sp", x1[0:NS, :], x1s_d, [], [b_x1])
                        gg, bgg = g2s, b_g2s
                    else:
                        i = grp * 2 + j
                        dma("sp", x1[:], x1_d[i * 128:(i + 1) * 128, :], [], [b_x1])
                        gg, bgg = g2b, b_g2b
                    tt("dve", x2[:npar], ps23[:npar, :], gg[:npar], ALU.mult, [b_ps23, bgg], [b_x2])
                    tt("dve", x2[:npar], x2[:npar], x1[:npar], ALU.add, [b_x2, b_x1], [b_x2])
                    rmsnorm_stats(npar, x2[:npar, :], b_x2, st4, junk, b_junk, b_st4, D)
                    stt(yo[:npar], x2[:npar], st4[:npar, 3:4], gfb[:npar], ALU.mult, ALU.mult,
                        [b_x2, b_st4, b_gfb], [b_yo])
                    if smp:
                        dma("sp", y_s, yo[0:NS, :], [b_yo], [], track=b_yo)
                    else:
                        dma("sp", y_own[i * 128:(i + 1) * 128, :], yo[:], [b_yo], [], track=b_yo)
        fw.enabled = True
        print("op counts", fw.cnt, "max dma cnt", max([b.dcnt for b in fw.dbufs] + [0]), "nsems", len(fw.sems), flush=True)
        fw.emit()
    return nc


def _constants(half):
    ident = np.eye(128, dtype=np.float32)
    kk = np.arange(128)
    tri = (kk[:, None] <= kk[None, :]).astype(np.float32)
    qi = np.arange(128)[:, None] + 128
    kj = np.arange(256)[None, :]
    dist = qi - kj
    band = (dist >= 0) & (dist <= 128)
    maskr = np.where(band, 0.0, NEG).astype(np.float32)
    maskfirst = np.where(band & (kj >= 128), 0.0, NEG).astype(np.float32)
    maske = maskr if half == 1 else maskfirst
    mneg = np.where(kk[None, :] >= kk[:, None], 0.0, NEG).astype(np.float32)
    sel = np.zeros((16, 16, 128), np.float32)
    for h in range(16):
        sel[h, h, :] = 1.0
    halo = np.full((128, 1), float(half), np.float32)
    inv_freq = (10000.0 ** (-np.arange(32, dtype=np.float32) / 32)).astype(np.float32)
    vt = np.arange(NTOK)
    pos = np.where(half == 1, vt, np.maximum(vt - S_OWN, 0)).astype(np.float32)
    pos = np.concatenate([pos, np.full(128, 8192.0, np.float32)])
    ang = pos[:, None] * inv_freq[None, :]
    cos = np.cos(ang).astype(np.float32)
    sin = np.sin(ang).astype(np.float32)
    tab = np.stack([np.concatenate([cos, cos], 1), np.concatenate([-sin, sin], 1)], 1)
    rope = np.ascontiguousarray(tab.reshape(33, 128, 2, 64).transpose(1, 0, 2, 3))
    return dict(k_ident=ident, k_tri=tri, k_maskr=maskr, k_maske=np.ascontiguousarray(maske), k_rope=rope,
                k_mneg=mneg, k_sel=sel, k_halo=halo)


_NC_CACHE = {}


def kernel(x_prompt, x_sample, c_prompt, c_sample, cache_k_win, cache_v_win, state_conv, state_ssm,
           w_ada, b_ada, w_in, conv_w, conv_b, dt_bias, a_log, d_skip, g_attn, g_ssm, w_out, w_up, w_down, g_final):
    f = lambda a: np.ascontiguousarray(np.asarray(a, dtype=np.float32))
    x_prompt, x_sample, c_prompt, c_sample = f(x_prompt), f(x_sample), f(c_prompt), f(c_sample)
    ck, cv = np.asarray(cache_k_win), np.asarray(cache_v_win)
    sconv, sssm = f(state_conv), f(state_ssm)
    shared = dict(
        w_ada=f(w_ada)[0], b_ada=f(b_ada)[0][None, :], w_in=f(w_in)[0],
        conv_wT=np.ascontiguousarray(f(conv_w)[0].reshape(4, 16, 128).transpose(2, 1, 0)),
        conv_bT=np.ascontiguousarray(f(conv_b)[0].reshape(16, 128).T),
        conv_w=f(conv_w)[0], conv_b=f(conv_b)[0][None, :],
        dt_bias=f(dt_bias)[0][None, :], a_log=f(a_log)[0][None, :], d_skip=f(d_skip)[0][None, :],
        g_col=np.ascontiguousarray(np.concatenate([f(g_attn)[0], f(g_ssm)[0]]).reshape(16, 128).T),
        w_out=f(w_out)[0], w_up=f(w_up)[0], w_down=f(w_down)[0], g_final=f(g_final)[None, :],
    )
    consts = [_constants(0), _constants(1)]
    in_maps = []
    for c in range(NCORES):
        b, half = c // 2, c % 2
        xt = np.zeros((NTOK, D), np.float32)
        if half == 1:
            xt[:] = x_prompt[b]
        else:
            xt[S_OWN:] = x_prompt[b, :S_OWN]
        sl = slice(c * NS, (c + 1) * NS)
        m = dict(shared)
        m.update(consts[half])
        m.update(
            x_tok=xt, x_s=np.ascontiguousarray(x_sample[sl, 0]),
            c_pT=np.ascontiguousarray(c_prompt[b].reshape(8, 128).T),
            c_sT=np.ascontiguousarray(c_sample[sl].reshape(NS, 8, 128).transpose(2, 1, 0)),
            cache_k=np.ascontiguousarray(ck[0, sl].reshape(NS, 2048, D), dtype=np.float32),
            cache_v=np.ascontiguousarray(cv[0, sl].reshape(NS, 2048, D), dtype=np.float32),
            st_conv=np.ascontiguousarray(sconv[0, sl]),
            st_ssm=np.ascontiguousarray(sssm[0, sl].reshape(NS, 1024, 128)),
        )
        in_maps.append(m)
    if _NC_CACHE.get("debug_in_maps_only"):
        return in_maps
    if "nc" not in _NC_CACHE:
        _NC_CACHE["nc"] = build_program()
    res = run_bass_kernel_spmd(_NC_CACHE["nc"], in_maps, core_ids=list(range(NCORES)))
    R = res.results
    y_prompt = np.stack([np.concatenate([R[2 * b]["y_own"], R[2 * b + 1]["y_own"]], 0) for b in range(4)])
    y_sample = np.concatenate([R[c]["y_s"] for c in range(NCORES)], 0)[:, None, :]
    kpo = np.stack([R[2 * b + 1]["kp"] for b in range(4)]).reshape(1, 4, 2048, 16, 64)
    vpo = np.stack([R[2 * b + 1]["vp"] for b in range(4)]).reshape(1, 4, 2048, 16, 64)
    cpo = np.stack([R[2 * b + 1]["convp"] for b in range(4)])[None]
    spo = np.stack([R[2 * b + 1]["ssmp"] for b in range(4)]).reshape(1, 4, 16, 64, 128)
    kso = np.concatenate([R[c]["ks"] for c in range(NCORES)], 0).reshape(1, 128, 2048, 16, 64)
    vso = np.concatenate([R[c]["vs"] for c in range(NCORES)], 0).reshape(1, 128, 2048, 16, 64)
    cso = np.concatenate([R[c]["convs"] for c in range(NCORES)], 0)[None]
    sso = np.concatenate([R[c]["ssms"] for c in range(NCORES)], 0).reshape(1, 128, 16, 64, 128)
    f32 = lambda a: np.ascontiguousarray(a, dtype=np.float32)
    return tuple(f32(a) for a in (y_prompt, y_sample, kpo, vpo, cpo, spo, kso, vso, cso, sso))
```
